# Optimizing a Trainium2 kernel written in Bass

```python
import math
import jax, jax.numpy as jnp
from jax import lax
import numpy as np

D_MODEL = 1024
BATCH = 8
SEQ = 4096
DEPTH = 2

GRID_W = 64
CTX_LEN = 256
N_MOD = 6
NORM_EPS = 1e-6
ROPE_THETA = 10000.0
Q_BLOCK = 128

HY_W = 256
HY_ORDER = 2
HY_BANDS = 16
HY_EMB = 1 + 2 * HY_BANDS
HY_FILT = 64
HY_CONV = 3
HY_TARGET = 1e-2
HY_FAST_PCT = 0.3
HY_SLOW_PCT = 1.5

GQA_HEADS = 6
GQA_KV_HEADS = 2
GQA_HEAD_DIM = 64

MLA_HEADS = 6
MLA_Q_RANK = 256
MLA_KV_RANK = 128
MLA_NOPE_DIM = 64
MLA_ROPE_DIM = 32
MLA_V_DIM = 64
MLA_QK_DIM = MLA_NOPE_DIM + MLA_ROPE_DIM

MIX_WIDTH = HY_W + GQA_HEADS * GQA_HEAD_DIM + MLA_HEADS * MLA_V_DIM
IN_SPLITS = (HY_W * (HY_ORDER + 1), GQA_HEADS * GQA_HEAD_DIM, GQA_KV_HEADS * GQA_HEAD_DIM,
             GQA_KV_HEADS * GQA_HEAD_DIM, MLA_Q_RANK, MLA_KV_RANK, MLA_ROPE_DIM)
IN_WIDTH = 1824
FFN_HIDDEN = ((8 * D_MODEL + 3 * 256 - 1) // (3 * 256)) * 256

kernel_name = "hybrid_hyena_gqa_mla_dit_block"


def rmsnorm(x, g):
    x32 = x.astype(jnp.float32)
    y = x32 * lax.rsqrt(jnp.mean(x32 * x32, axis=-1, keepdims=True) + NORM_EPS)
    return (y * g.astype(jnp.float32)).astype(x.dtype)


def split_cols(u, sizes):
    return jnp.split(u, np.cumsum(sizes)[:-1].tolist(), axis=-1)


def short_conv(u, w, b):
    up = jnp.pad(u, ((0, 0), (1, 1), (0, 0)))
    return up[:, :-2] * w[0] + up[:, 1:-1] * w[1] + up[:, 2:] * w[2] + b


def hyena_kernels(L, w1, b1, w2, b2, w3, freq):
    t = jnp.linspace(0.0, 1.0, L, dtype=jnp.float32)[:, None]
    wpos = 2.0 * math.pi * jnp.arange(L, dtype=jnp.float32)[:, None] / L
    f = jnp.linspace(1e-4, HY_BANDS - 1, HY_BANDS, dtype=jnp.float32)[None, :]
    z = jnp.concatenate([t, jnp.cos(f * wpos), -jnp.sin(f * wpos)], axis=-1)
    h = jnp.sin(freq * (z @ w1 + b1))
    h = jnp.sin(freq * (h @ w2 + b2))
    h = (h @ w3).astype(jnp.float32).reshape(L, HY_ORDER, 2, HY_W)
    deltas = jnp.abs(jnp.linspace(math.log(HY_TARGET) / HY_FAST_PCT, math.log(HY_TARGET) / HY_SLOW_PCT,
                                  HY_W, dtype=jnp.float32))
    h = h * jnp.exp(-t * deltas)[:, None, None, :]
    fwd, bwd = h[:, :, 0], h[:, :, 1]
    k = jnp.concatenate([fwd, jnp.zeros_like(fwd[:1]), bwd[:0:-1]], axis=0)
    return k / jnp.sum(jnp.abs(k), axis=0, keepdims=True)


def long_conv(z, k, bias):
    L = z.shape[1]
    z32 = z.astype(jnp.float32)
    Z = jnp.fft.rfft(z32, n=2 * L, axis=1)
    K = jnp.fft.rfft(k, n=2 * L, axis=0)
    y = jnp.fft.irfft(Z * K[None], n=2 * L, axis=1)[:, :L]
    return (y + z32 * bias.astype(jnp.float32)).astype(z.dtype)


def hyena_mixer(u, lp):
    L = u.shape[1]
    u = short_conv(u, lp["hy_conv_w"], lp["hy_conv_b"])
    v, x1, x2 = jnp.split(u, 3, axis=-1)
    k = hyena_kernels(L, lp["hy_filt_w1"], lp["hy_filt_b1"], lp["hy_filt_w2"], lp["hy_filt_b2"],
                      lp["hy_filt_w3"], lp["hy_filt_freq"])
    z = x1 * long_conv(v, k[:, 0], lp["hy_bias"][0])
    return x2 * long_conv(z, k[:, 1], lp["hy_bias"][1])


def _rope_1d(x, pos):
    d = x.shape[-1]
    inv = ROPE_THETA ** (-jnp.arange(0, d, 2, dtype=jnp.float32) / d)
    ang = pos.astype(jnp.float32)[:, None] * inv[None, :]
    ang = jnp.concatenate([ang, ang], axis=-1)[None, :, None, :]
    x32 = x.astype(jnp.float32)
    x1, x2 = jnp.split(x32, 2, axis=-1)
    rot = jnp.concatenate([-x2, x1], axis=-1)
    return (x32 * jnp.cos(ang) + rot * jnp.sin(ang)).astype(x.dtype)


def rope_2d(x, row, col):
    xr, xc = jnp.split(x, 2, axis=-1)
    return jnp.concatenate([_rope_1d(xr, row), _rope_1d(xc, col)], axis=-1)


def attn_queries(gq, mq, lp, row, col):
    B, L, _ = gq.shape
    qg = rmsnorm(gq.reshape(B, L, GQA_HEADS, GQA_HEAD_DIM), lp["gqa_q_g"])
    qm = (rmsnorm(mq, lp["mla_q_g"]) @ lp["mla_w_uq"]).reshape(B, L, MLA_HEADS, MLA_QK_DIM)
    qm_nope, qm_pe = jnp.split(qm, [MLA_NOPE_DIM], axis=-1)
    if row is not None:
        qg = rope_2d(qg, row, col)
        qm_pe = rope_2d(qm_pe, row, col)
    qm = jnp.concatenate([qm_nope, qm_pe], axis=-1)
    qg = qg.reshape(B, L, GQA_KV_HEADS, GQA_HEADS // GQA_KV_HEADS, GQA_HEAD_DIM)
    return qg, qm[:, :, :, None, :]


def attn_keys_values(gk, gv, mkv, mkr, lp, row, col):
    B, L, _ = gk.shape
    kg = rmsnorm(gk.reshape(B, L, GQA_KV_HEADS, GQA_HEAD_DIM), lp["gqa_k_g"])
    vg = gv.reshape(B, L, GQA_KV_HEADS, GQA_HEAD_DIM)
    kv = (rmsnorm(mkv, lp["mla_kv_g"]) @ lp["mla_w_ukv"]).reshape(B, L, MLA_HEADS, MLA_NOPE_DIM + MLA_V_DIM)
    km_nope, vm = jnp.split(kv, [MLA_NOPE_DIM], axis=-1)
    km_pe = mkr[:, :, None, :]
    if row is not None:
        kg = rope_2d(kg, row, col)
        km_pe = rope_2d(km_pe, row, col)
    km = jnp.concatenate([km_nope, jnp.broadcast_to(km_pe, (B, L, MLA_HEADS, MLA_ROPE_DIM))], axis=-1)
    return kg, vg, km, vm


def blocked_attention(q, k, v, scale):
    B, L, KH, G, Dq = q.shape
    nb = L // Q_BLOCK
    qb = jnp.moveaxis(q.reshape(B, nb, Q_BLOCK, KH, G, Dq), 1, 0)
    k32 = k.astype(jnp.float32)
    v32 = v.astype(jnp.float32)

    def one_block(qblk):
        s = jnp.einsum("bqkgd,bskd->bkgqs", qblk.astype(jnp.float32), k32) * scale
        p = jax.nn.softmax(s, axis=-1)
        return jnp.einsum("bkgqs,bskd->bqkgd", p, v32).astype(v.dtype)

    o = lax.map(one_block, qb)
    return jnp.moveaxis(o, 0, 1).reshape(B, L, KH * G * v.shape[-1])


def attend(q, kv):
    qg, qm = q
    kg, vg, km, vm = kv
    yg = blocked_attention(qg, kg, vg, GQA_HEAD_DIM ** -0.5)
    ym = blocked_attention(qm, km, vm, MLA_QK_DIM ** -0.5)
    return yg, ym


def merge_heads(y_hy, y_att, w_out):
    return jnp.concatenate([y_hy, y_att[0], y_att[1]], axis=-1) @ w_out


def swiglu(h, w1, w3, w2):
    return (jax.nn.silu(h @ w1) * (h @ w3)) @ w2


def setup_inputs(seed: int = 0) -> dict:
    key = jax.random.key(seed)
    ks = iter(jax.random.split(key, 32))

    def nrm(shape, scale):
        return jax.random.normal(next(ks), shape, jnp.float32) * scale

    def gain(shape):
        return 1.0 + nrm(shape, 0.05)

    D = D_MODEL
    return {
        "x": nrm((BATCH, SEQ, D), 1.0),
        "c": nrm((BATCH, D), 1.0),
        "ctx": nrm((BATCH, CTX_LEN, D), 1.0),
        "c_ctx": nrm((D,), 1.0),
        "mod_w": nrm((DEPTH, D, N_MOD * D), 0.5 * D ** -0.5),
        "mod_b": nrm((DEPTH, N_MOD * D), 0.01),
        "norm1_g": gain((DEPTH, D)),
        "norm2_g": gain((DEPTH, D)),
        "w_in": nrm((DEPTH, D, IN_WIDTH), D ** -0.5),
        "hy_conv_w": nrm((DEPTH, HY_CONV, HY_W * (HY_ORDER + 1)), HY_CONV ** -0.5),
        "hy_conv_b": nrm((DEPTH, HY_W * (HY_ORDER + 1)), 0.01),
        "hy_filt_w1": nrm((DEPTH, HY_EMB, HY_FILT), HY_EMB ** -0.5),
        "hy_filt_b1": nrm((DEPTH, HY_FILT), 0.1),
        "hy_filt_w2": nrm((DEPTH, HY_FILT, HY_FILT), HY_FILT ** -0.5),
        "hy_filt_b2": nrm((DEPTH, HY_FILT), 0.1),
        "hy_filt_w3": nrm((DEPTH, HY_FILT, HY_ORDER * 2 * HY_W), HY_FILT ** -0.5),
        "hy_filt_freq": gain((DEPTH, HY_FILT)),
        "hy_bias": nrm((DEPTH, HY_ORDER, HY_W), 0.5),
        "gqa_q_g": gain((DEPTH, GQA_HEAD_DIM)),
        "gqa_k_g": gain((DEPTH, GQA_HEAD_DIM)),
        "mla_q_g": gain((DEPTH, MLA_Q_RANK)),
        "mla_kv_g": gain((DEPTH, MLA_KV_RANK)),
        "mla_w_uq": nrm((DEPTH, MLA_Q_RANK, MLA_HEADS * MLA_QK_DIM), MLA_Q_RANK ** -0.5),
        "mla_w_ukv": nrm((DEPTH, MLA_KV_RANK, MLA_HEADS * (MLA_NOPE_DIM + MLA_V_DIM)), MLA_KV_RANK ** -0.5),
        "w_out": nrm((DEPTH, MIX_WIDTH, D), MIX_WIDTH ** -0.5),
        "ffn_w1": nrm((DEPTH, D, FFN_HIDDEN), D ** -0.5),
        "ffn_w3": nrm((DEPTH, D, FFN_HIDDEN), D ** -0.5),
        "ffn_w2": nrm((DEPTH, FFN_HIDDEN, D), FFN_HIDDEN ** -0.5),
        "final_g": gain((D,)),
    }


def reference(x, c, ctx, c_ctx, mod_w, mod_b, norm1_g, norm2_g, w_in, hy_conv_w, hy_conv_b,
              hy_filt_w1, hy_filt_b1, hy_filt_w2, hy_filt_b2, hy_filt_w3, hy_filt_freq, hy_bias,
              gqa_q_g, gqa_k_g, mla_q_g, mla_kv_g, mla_w_uq, mla_w_ukv, w_out,
              ffn_w1, ffn_w3, ffn_w2, final_g):
    B, n_lat, D = x.shape
    n_rows = n_lat // GRID_W
    row = jnp.repeat(jnp.arange(n_rows, dtype=jnp.int32), GRID_W)
    col = jnp.tile(jnp.arange(GRID_W, dtype=jnp.int32), n_rows)
    off = np.cumsum((0,) + IN_SPLITS).tolist()

    x_lat, x_ctx = x, ctx
    sc = jax.nn.silu(c)
    sc_ctx = jax.nn.silu(c_ctx)
    for l in range(DEPTH):
        last = l == DEPTH - 1
        lp = {
            "hy_conv_w": hy_conv_w[l], "hy_conv_b": hy_conv_b[l],
            "hy_filt_w1": hy_filt_w1[l], "hy_filt_b1": hy_filt_b1[l],
            "hy_filt_w2": hy_filt_w2[l], "hy_filt_b2": hy_filt_b2[l],
            "hy_filt_w3": hy_filt_w3[l], "hy_filt_freq": hy_filt_freq[l], "hy_bias": hy_bias[l],
            "gqa_q_g": gqa_q_g[l], "gqa_k_g": gqa_k_g[l],
            "mla_q_g": mla_q_g[l], "mla_kv_g": mla_kv_g[l],
            "mla_w_uq": mla_w_uq[l], "mla_w_ukv": mla_w_ukv[l],
        }
        w_in_l, w_out_l = w_in[l], w_out[l]

        mod = (sc @ mod_w[l] + mod_b[l]).reshape(B, N_MOD, D)
        shift1, scale1, gate1, shift2, scale2, gate2 = [mod[:, i, None, :] for i in range(N_MOD)]
        n_ctx_mod = 2 if last else N_MOD
        mod_c = (sc_ctx @ mod_w[l][:, :n_ctx_mod * D] + mod_b[l][:n_ctx_mod * D]).reshape(n_ctx_mod, D)

        h_c = rmsnorm(x_ctx, norm1_g[l]) * (1.0 + mod_c[1]) + mod_c[0]
        if last:
            w_kv = jnp.concatenate([w_in_l[:, off[2]:off[4]], w_in_l[:, off[5]:off[7]]], axis=1)
            gk_c, gv_c, mkv_c, mkr_c = split_cols(h_c @ w_kv, IN_SPLITS[2:4] + IN_SPLITS[5:7])
            kv_c = attn_keys_values(gk_c, gv_c, mkv_c, mkr_c, lp, None, None)
        else:
            hy_c, gq_c, gk_c, gv_c, mq_c, mkv_c, mkr_c = split_cols(h_c @ w_in_l, IN_SPLITS)
            kv_c = attn_keys_values(gk_c, gv_c, mkv_c, mkr_c, lp, None, None)
            q_c = attn_queries(gq_c, mq_c, lp, None, None)
            y_c = merge_heads(hyena_mixer(hy_c, lp), attend(q_c, kv_c), w_out_l)
            x_ctx_mid = x_ctx + mod_c[2] * y_c

        h = rmsnorm(x_lat, norm1_g[l]) * (1.0 + scale1) + shift1
        hy, gq, gk, gv, mq, mkv, mkr = split_cols(h @ w_in_l, IN_SPLITS)
        q = attn_queries(gq, mq, lp, row, col)
        kv = attn_keys_values(gk, gv, mkv, mkr, lp, row, col)
        kv_all = tuple(jnp.concatenate([a_c, a_l], axis=1) for a_c, a_l in zip(kv_c, kv))
        y = merge_heads(hyena_mixer(hy, lp), attend(q, kv_all), w_out_l)
        x_lat = x_lat + gate1 * y

        h2 = rmsnorm(x_lat, norm2_g[l]) * (1.0 + scale2) + shift2
        x_lat = x_lat + gate2 * swiglu(h2, ffn_w1[l], ffn_w3[l], ffn_w2[l])
        if not last:
            h2_c = rmsnorm(x_ctx_mid, norm2_g[l]) * (1.0 + mod_c[4]) + mod_c[3]
            x_ctx = x_ctx_mid + mod_c[5] * swiglu(h2_c, ffn_w1[l], ffn_w3[l], ffn_w2[l])

    return rmsnorm(x_lat, final_g)
```

```python
import contextlib
import math
import os
import numpy as np
import concourse.bass as bass
import concourse.mybir as mybir
from concourse.bass_utils import run_bass_kernel_spmd

F32 = mybir.dt.float32
BF16 = mybir.dt.bfloat16
I32 = mybir.dt.int32
AF = mybir.ActivationFunctionType
ALU = mybir.AluOpType
AX = mybir.AxisListType

D = 1024
NL = 4096
NCX = 256
NT = NL + NCX
DEPTH = 2
FH = 2816
NJ = 22
EPS = 1e-6
ENGS = ("pe", "act", "dve", "pool", "sp")


class Sched:
    def __init__(self, nc, n_dma_sems=12, same_engine_sync=("act", "dve", "pool")):
        self.nc = nc
        self.ops = []
        self.last_w = {}
        self.readers = {}
        self.n_dma_sems = n_dma_sems
        self.same_engine_sync = same_engine_sync
        self.since_barrier = []

    @staticmethod
    def _flat(k):
        out = []

        def rec(x):
            if isinstance(x, tuple):
                for y in x:
                    rec(y)
            else:
                out.append(x)
        rec(k)
        return tuple(out)

    def _related(self, k):
        res = set()
        for i in range(1, len(k) + 1):
            p = k[:i]
            if p in self.last_w or p in self.readers:
                res.add(p)
        res |= self.children.get(k, set())
        return res

    def _register(self, k):
        for i in range(1, len(k) + 1):
            self.children.setdefault(k[:i], set()).add(k)

    def op(self, eng, fn, reads=(), writes=(), dma=False, extra_deps=()):
        if not hasattr(self, "children"):
            self.children = {}
        oid = len(self.ops)
        deps = set(extra_deps)
        reads = [self._flat(k) for k in reads]
        writes = [self._flat(k) for k in writes]
        for k in reads:
            for r in self._related(k):
                w = self.last_w.get(r)
                if w is not None:
                    deps.add(w)
        for k in writes:
            for r in self._related(k):
                w = self.last_w.get(r)
                if w is not None:
                    deps.add(w)
                for rd in self.readers.get(r, ()):
                    deps.add(rd)
        deps.discard(oid)
        self.ops.append(dict(eng=eng, fn=fn, deps=deps, dma=dma))
        for k in reads:
            self.readers.setdefault(k, []).append(oid)
            self._register(k)
        for k in writes:
            for c in list(self.children.get(k, ())):
                if c != k:
                    self.last_w.pop(c, None)
                    self.readers.pop(c, None)
            self.last_w[k] = oid
            self.readers[k] = []
            self._register(k)
        self.since_barrier.append(oid)
        return oid

    def dma(self, eng, out, in_, reads=(), writes=()):
        return self.op(eng, lambda e: e.dma_start(out=out, in_=in_), reads, writes, dma=True)

    def barrier(self):
        last = {}
        dmas = []
        for i in self.since_barrier:
            o = self.ops[i]
            if o["dma"]:
                dmas.append(i)
            else:
                last[o["eng"]] = i
        deps = list(last.values()) + dmas
        self.since_barrier = []
        for e in ENGS:
            self.op(e, lambda en: en.nop(), extra_deps=deps)
        self.last_w = {}
        self.readers = {}
        self.children = {}

    def emit(self, final_wait_eng="sp"):
        nc = self.nc
        ops = self.ops
        need = [False] * len(ops)
        for o in ops:
            for d in o["deps"]:
                p = ops[d]
                if p["dma"] or p["eng"] != o["eng"] or o["dma"] or (o["eng"] in self.same_engine_sync):
                    need[d] = True
        for i, o in enumerate(ops):
            if o["dma"]:
                need[i] = True
        eng_cnt = {e: 0 for e in ENGS}
        dma_cnt = {e: 0 for e in ENGS}
        token = [None] * len(ops)
        gate = [None] * len(ops)
        for i, o in enumerate(ops):
            if not need[i]:
                continue
            e = o["eng"]
            if o["dma"]:
                j = dma_cnt[e] % self.n_dma_sems
                u = dma_cnt[e] // self.n_dma_sems + 1
                dma_cnt[e] += 1
                token[i] = (("dma", e, j), 16 * u)
                if u > 1:
                    gate[i] = (("dma", e, j), 16 * (u - 1))
            else:
                eng_cnt[e] += 1
                ep = eng_cnt[e] // 30000
                token[i] = (("eng", e, ep), eng_cnt[e] - ep * 30000 + (1 if ep else 0))
        sem_names = sorted({t[0] for t in token if t is not None}, key=str)
        with contextlib.ExitStack() as st:
            sems = {}
            for sn in sem_names:
                sems[sn] = st.enter_context(nc.semaphore("_".join(str(x) for x in sn)))
            block = st.enter_context(nc.Block())
            per_eng = {e: [i for i, o in enumerate(ops) if o["eng"] == e] for e in ENGS}
            dma_tokens = [token[i] for i, o in enumerate(ops) if o["dma"]]

            def make(e):
                def body(engine):
                    known = {}

                    def wait(tok):
                        sn, v = tok
                        if known.get(sn, 0) >= v:
                            return
                        engine.wait_ge(sems[sn], v)
                        known[sn] = v

                    for i in per_eng[e]:
                        o = ops[i]
                        for d in sorted(o["deps"]):
                            p = ops[d]
                            if token[d] is None:
                                continue
                            if (not p["dma"]) and p["eng"] == e and (not o["dma"]) and e not in self.same_engine_sync:
                                continue
                            wait(token[d])
                        if gate[i] is not None:
                            wait(gate[i])
                        ins = o["fn"](engine)
                        if token[i] is not None:
                            sn, v = token[i]
                            ins.then_inc(sems[sn], 16 if o["dma"] else 1)
                    if e == final_wait_eng:
                        final = {}
                        for sn, v in dma_tokens:
                            final[sn] = max(final.get(sn, 0), v)
                        for sn, v in final.items():
                            wait((sn, v))
                return body

            block.tensor(make("pe"))
            block.scalar(make("act"))
            block.vector(make("dve"))
            block.gpsimd(make("pool"))
            block.sync(make("sp"))


def fft_consts(N1):
    N = N1 * 128
    T = N1 // 2
    NK1 = N1 // 2 + 1
    t = np.arange(T)[:, None]
    k1 = np.arange(NK1)[None, :]
    th = 2 * np.pi * t * k1 / N1
    F1 = np.stack([np.cos(th), -np.sin(th)], axis=-1).reshape(T, 2 * NK1)
    p = np.arange(128)[:, None, None]
    kk1 = np.arange(NK1)[None, :, None]
    k2 = np.arange(128)[None, None, :]
    ph = 2 * np.pi * ((p * (kk1 + N1 * k2)) % N) / N
    Mr, Mi = np.cos(ph), -np.sin(ph)
    M2c = np.stack([Mr, Mi, -Mi], axis=2)
    phT = np.transpose(ph, (2, 1, 0))
    Dr, Di = np.cos(phT), np.sin(phT)
    D2c = np.stack([Dr, Di, -Di], axis=2)
    w = np.full(NK1, 2.0)
    w[0] = 1
    w[-1] = 1
    tt = np.arange(T)[None, :]
    kk = np.arange(NK1)[:, None]
    th2 = 2 * np.pi * tt * kk / N1
    G = np.stack([w[:, None] / N * np.cos(th2), -w[:, None] / N * np.sin(th2)], axis=1).reshape(2 * NK1, T)
    M2s = np.ascontiguousarray(np.transpose(M2c, (1, 0, 2, 3))).astype(np.float32)
    D2s = np.ascontiguousarray(np.transpose(D2c, (1, 0, 2, 3))).astype(np.float32)
    return F1.astype(np.float32), M2s, D2s, G.astype(np.float32)


def rope_tables():
    n_rows = NL // 64
    row = np.repeat(np.arange(n_rows), 64).astype(np.float64)
    col = np.tile(np.arange(64), n_rows).astype(np.float64)

    def tab(dh, pos):
        inv = 10000.0 ** (-np.arange(0, dh, 2, dtype=np.float64) / dh)
        ang = pos[:, None] * inv[None, :]
        ang = np.concatenate([ang, ang], axis=-1)
        c = np.cos(ang)
        s = np.sin(ang)
        s[:, : dh // 2] *= -1.0
        return c.T, s.T

    def full(dh):
        cr, sr = tab(dh, row)
        cc, sc = tab(dh, col)
        c = np.concatenate([cr, cc], 0)
        s = np.concatenate([sr, sc], 0)
        c = np.concatenate([np.ones((2 * dh, NCX)), c], 1)
        s = np.concatenate([np.zeros((2 * dh, NCX)), s], 1)
        return c, s

    cg, sg = full(32)
    cosG = np.concatenate([cg, cg], 0)
    sinG = np.concatenate([sg, sg], 0)
    cm, sm = full(16)
    cosM = np.concatenate([np.ones((64, NT)), cm], 0)
    sinM = np.concatenate([np.zeros((64, NT)), sm], 0)

    def swap(dh):
        P = np.zeros((2 * dh, 2 * dh))
        h = dh // 2
        for b in range(2):
            for i in range(h):
                P[b * dh + i, b * dh + h + i] = 1
                P[b * dh + h + i, b * dh + i] = 1
        return P

    PG = np.zeros((128, 128))
    PG[0:64, 0:64] = swap(32)
    PG[64:128, 64:128] = swap(32)
    PM = np.zeros((96, 96))
    PM[64:96, 64:96] = swap(16)
    f = lambda a: np.ascontiguousarray(a).astype(np.float32)
    return f(cosG), f(sinG), f(cosM), f(sinM), f(PG), f(PM)


def hy_embed(L):
    t = np.linspace(0.0, 1.0, L, dtype=np.float32)[:, None]
    wpos = 2.0 * math.pi * np.arange(L, dtype=np.float32)[:, None] / L
    f = np.linspace(1e-4, 16 - 1, 16, dtype=np.float32)[None, :]
    z = np.concatenate([t, np.cos(f * wpos), -np.sin(f * wpos)], axis=-1)
    return np.ascontiguousarray(z.T).astype(np.float32), np.ascontiguousarray(t.T).astype(np.float32)


HY_DELTAS = np.abs(np.linspace(math.log(1e-2) / 0.3, math.log(1e-2) / 1.5, 256, dtype=np.float32))


class StopBuild(Exception):
    pass


class KB:
    def __init__(self, dbg=False, stop_after=None):
        self.dbg = dbg
        self.stop_after = stop_after
        self.nc = bass.Bass("TRN2", target_bir_lowering=False)
        self.S = Sched(self.nc)
        self.st = contextlib.ExitStack()
        self.inputs = {}
        self.uid = 0

    def din(self, name, shape, dt=F32):
        ap = self.nc.dram_tensor(name, list(shape), dt, kind="ExternalInput").ap()
        self.inputs[name] = ap
        return ap

    def chk(self, tag):
        if self.stop_after == tag:
            raise StopBuild()

    def dscr(self, name, shape, dt):
        kind = "ExternalOutput" if (self.dbg and not name.startswith(("w1S", "w3S", "w2S", "winS", "woutS", "M2S", "D2S", "wuqS", "wukvS"))) else "Internal"
        return self.nc.dram_tensor(name, list(shape), dt, kind=kind).ap()

    def alloc_reset(self, off=None):
        self.off = self.persist if off is None else off

    def A(self, cols_f32):
        a = self.arena[:, self.off:self.off + cols_f32]
        self.off += cols_f32
        assert self.off <= self.arena_cols, ("sbuf overflow", self.off)
        return a

    def Fm(self, cols):
        self.uid += 1
        return self.A(cols), ("sb", self.uid)

    def Bm(self, cols):
        self.uid += 1
        assert cols % 2 == 0
        return self.A(cols // 2).bitcast(BF16), ("sb", self.uid)

    def phase_end(self):
        self.S.barrier()
        self.alloc_reset()

    def mm(self, out, lhsT, rhs, start, stop, reads, writes):
        self.S.op("pe", lambda e: e.matmul(out, lhsT, rhs, start=start, stop=stop), reads, writes)

    def build(self):
        nc, S, st = self.nc, self.S, self.st
        xin = self.din("xin", [NT, D])
        cvec = self.din("cvec", [128, 8, 2])
        mod_w = self.din("mod_w", [DEPTH, D, 6 * D])
        modbT = self.din("modbT", [DEPTH, 128, 48])
        mod_b = self.din("mod_b", [DEPTH, 6 * D])
        g1T = self.din("g1T", [DEPTH, 128, 8])
        g2T = self.din("g2T", [DEPTH, 128, 8])
        w_in = self.din("w_in", [DEPTH, D, 1824])
        cwT = self.din("cwT", [DEPTH, 128, 6, 3])
        cbT = self.din("cbT", [DEPTH, 128, 6])
        hbT = self.din("hbT", [DEPTH, 128, 4])
        fw1 = self.din("fw1", [DEPTH, 33, 64])
        fb1 = self.din("fb1", [DEPTH, 64, 1])
        fw2 = self.din("fw2", [DEPTH, 64, 64])
        fb2 = self.din("fb2", [DEPTH, 64, 1])
        fw3 = self.din("fw3", [DEPTH, 64, 1024])
        ffr = self.din("ffr", [DEPTH, 64, 1])
        gqT = self.din("gqT", [DEPTH, 128, 1])
        gkT = self.din("gkT", [DEPTH, 128, 1])
        mqgT = self.din("mqgT", [DEPTH, 128, 2])
        mkvgT = self.din("mkvgT", [DEPTH, 128, 1])
        w_uq = self.din("w_uq", [DEPTH, 256, 576])
        w_ukv = self.din("w_ukv", [DEPTH, 128, 768])
        w_out = self.din("w_out", [DEPTH, D, D])
        ffn_w1 = self.din("ffn_w1", [DEPTH, D, FH])
        ffn_w3 = self.din("ffn_w3", [DEPTH, D, FH])
        ffn_w2 = self.din("ffn_w2", [DEPTH, FH, D])
        fing = self.din("fing", [128, D])
        cosG = self.din("cosG", [128, NT]); sinG = self.din("sinG", [128, NT])
        cosM = self.din("cosM", [96, NT]); sinM = self.din("sinM", [96, NT])
        PGd = self.din("PG", [128, 128]); PMd = self.din("PM", [96, 96])
        zT = {4096: self.din("zT_l", [33, 4096]), 256: self.din("zT_c", [33, 256])}
        tl = {4096: self.din("tl_l", [1, 4096]), 256: self.din("tl_c", [1, 256])}
        ndelta = self.din("ndelta", [128, 2])
        fc = {}
        for tag, N1 in (("l", 64), ("c", 4)):
            T = N1 // 2; NK1 = N1 // 2 + 1
            fc[N1] = dict(T=T, NK1=NK1, F1=self.din("F1" + tag, [T, 2 * NK1]), M2=self.din("M2" + tag, [NK1, 128, 3, 128]),
                          D2=self.din("D2" + tag, [NK1, 128, 3, 128]), G=self.din("G" + tag, [2 * NK1, T]))
        yout = nc.dram_tensor("yout", [NL, D], F32, kind="ExternalOutput").ap()
        xres = self.dscr("xres", [NT, D], F32)
        uT = self.dscr("uT", [1056, NT], F32)
        hyT = self.dscr("hyT", [768, NT], F32)
        cvy = self.dscr("cvy", [256, NT], F32)
        cvz = self.dscr("cvz", [256, NT], F32)
        kTs = self.dscr("kTs", [1024, NL], F32)
        Vall = self.dscr("Vall", [NT, 8, 64], BF16)
        QT = self.dscr("QT", [12, 96, NT], BF16)
        KT = self.dscr("KT", [8, 96, NT], BF16)
        yT = self.dscr("yT", [1024, NT], BF16)
        winS = self.dscr("winS", [DEPTH, 15, 128, 8, 128], BF16)
        woutS = self.dscr("woutS", [DEPTH, 128, 8, D], BF16)
        w1S = self.dscr("w1S", [DEPTH, NJ, 128, 8, 128], BF16)
        w3S = self.dscr("w3S", [DEPTH, NJ, 128, 8, 128], BF16)
        w2S = self.dscr("w2S", [DEPTH, 128, NJ, D], BF16)
        wuqS = self.dscr("wuqS", [DEPTH, 128, 2, 576], BF16)
        wukvS = self.dscr("wukvS", [DEPTH, 128, 768], BF16)
        M2S = {N1: self.dscr("M2S%d" % N1, [fc[N1]["NK1"], 128, 3, 128], BF16) for N1 in (64, 4)}
        D2S = {N1: self.dscr("D2S%d" % N1, [fc[N1]["NK1"], 128, 3, 128], BF16) for N1 in (64, 4)}

        self.arena_cols = 47 * 1024
        self.arena = st.enter_context(nc.sbuf_tensor("arena", [128, self.arena_cols], F32))
        self.ps = [st.enter_context(nc.psum_tensor("ps%d" % i, [128, 512], F32)) for i in range(8)]
        ps = self.ps
        PK = [("ps", i) for i in range(8)]
        self.off = 0
        identf, k_identf = self.Fm(128)
        ident, k_ident = self.Bm(128)
        onesf, k_onesf = self.Fm(128)
        bd64, k_bd64 = self.Fm(128)
        epsc, k_eps = self.Fm(1)
        S.op("pool", lambda e: e.memset(identf, 1.0), writes=[k_identf])
        S.op("pool", lambda e: e.affine_select(identf, identf, [[-1, 128]], ALU.is_equal, 0.0, base=0, channel_multiplier=1), reads=[k_identf], writes=[k_identf])
        S.op("dve", lambda e: e.tensor_copy(ident, identf), reads=[k_identf], writes=[k_ident])
        S.op("pool", lambda e: e.memset(onesf, 1.0), writes=[k_onesf])
        S.op("pool", lambda e: e.memset(bd64, 0.0), writes=[k_bd64])
        S.op("pool", lambda e: e.memset(bd64[0:64, 0:64], 1.0), reads=[k_bd64], writes=[k_bd64])
        S.op("pool", lambda e: e.memset(bd64[64:128, 64:128], 1.0), reads=[k_bd64], writes=[k_bd64])
        S.op("pool", lambda e: e.memset(epsc, EPS), writes=[k_eps])
        A1, k_A1 = self.Fm(16); B1, k_B1 = self.Fm(16); A2, k_A2 = self.Fm(16); B2, k_B2 = self.Fm(16)
        A1 = A1.rearrange("p (k m) -> p k m", m=2); B1 = B1.rearrange("p (k m) -> p k m", m=2)
        A2 = A2.rearrange("p (k m) -> p k m", m=2); B2 = B2.rearrange("p (k m) -> p k m", m=2)
        gate = {}
        for gi in (1, 2):
            for m in (0, 1):
                gate[gi, m] = self.Fm(D)
        finalg, k_fing = self.Fm(D)
        S.dma("sp", finalg, fing, writes=[k_fing])
        rn, k_rn = self.Fm(4)
        hb, k_hb = self.Fm(4)
        self.persist = self.off

        for l in range(DEPTH):
            for oc in range(15):
                mw = 128 if oc < 14 else 32
                S.dma("pool", winS[l, oc, :, :, 0:mw], w_in[l][:, oc * 128:oc * 128 + mw].rearrange("(k p) m -> p k m", p=128), writes=[("winS", l, oc)])
            for h2 in range(2):
                S.dma("pool", woutS[l, :, :, h2 * 512:(h2 + 1) * 512], w_out[l][:, h2 * 512:(h2 + 1) * 512].rearrange("(k p) n -> p k n", p=128), writes=[("woutS", l, h2)])
            for j in range(NJ):
                S.dma("pool", w1S[l, j], ffn_w1[l][:, j * 128:(j + 1) * 128].rearrange("(k p) m -> p k m", p=128), writes=[("w1S", l, j)])
                S.dma("pool", w3S[l, j], ffn_w3[l][:, j * 128:(j + 1) * 128].rearrange("(k p) m -> p k m", p=128), writes=[("w3S", l, j)])
            for h2 in range(2):
                S.dma("pool", w2S[l, :, :, h2 * 512:(h2 + 1) * 512], ffn_w2[l][:, h2 * 512:(h2 + 1) * 512].rearrange("(j p) n -> p j n", p=128), writes=[("w2S", l, h2)])
            S.dma("pool", wuqS[l], w_uq[l].rearrange("(k p) n -> p k n", p=128), writes=[("wuqS", l)])
            S.dma("pool", wukvS[l], w_ukv[l], writes=[("wukvS", l)])
        for N1 in (64, 4):
            for k1 in range(fc[N1]["NK1"]):
                S.dma("pool", M2S[N1][k1], fc[N1]["M2"][k1], writes=[("M2S", N1, k1)])
                S.dma("pool", D2S[N1][k1], fc[N1]["D2"][k1], writes=[("D2S", N1, k1)])
        S.barrier()

        tok_chunks = [(0, NCX)] + [(NCX + 512 * i, 512) for i in range(8)]
        rr = [0]

        def evac_eng():
            rr[0] += 1
            return "act" if rr[0] % 2 else "dve"

        def copy_op(eng, out, in_, reads, writes):
            if eng == "act":
                S.op("act", lambda e: e.activation(out, in_, AF.Copy), reads, writes)
            else:
                S.op(eng, lambda e: e.tensor_copy(out, in_), reads, writes)

        def rstd_from_ss(ss, k_ss, rs, k_rs, inv_d):
            n = ss.partition_size()
            S.op("act", lambda e: e.activation(rs, ss, AF.Sqrt, bias=epsc[0:n, :], scale=inv_d), reads=[k_ss, k_eps], writes=[k_rs])
            S.op("dve", lambda e: e.reciprocal(rs, rs), reads=[k_rs], writes=[k_rs])

        def run_layers():
            for l in range(DEPTH):
                last = (l == DEPTH - 1)
                Xl = xin if l == 0 else xres
                def phaseM():
                    self.alloc_reset()
                    cv, k_cv = self.Fm(16)
                    scb, k_scb = self.Bm(16)
                    scB, k_scB = self.Bm(8 * 2 * 128)
                    cv3 = cv.rearrange("p (k m) -> p k m", m=2)
                    scb3 = scb.rearrange("p (k m) -> p k m", m=2)
                    scB4 = scB.rearrange("p (k m c) -> p k m c", k=8, m=2)
                    mbT, k_mbT = self.Fm(48)
                    g1s, k_g1s = self.Fm(8); g2s, k_g2s = self.Fm(8)
                    modT, k_modT = self.Fm(96)
                    modT3 = modT.rearrange("p (j m) -> p j m", m=2)
                    mbbc, k_mbbc = self.Fm(D)
                    S.dma("sp", cv3, cvec, writes=[k_cv])
                    S.dma("sp", mbT, modbT[l], writes=[k_mbT])
                    S.dma("sp", g1s, g1T[l], writes=[k_g1s])
                    S.dma("sp", g2s, g2T[l], writes=[k_g2s])
                    S.op("act", lambda e: e.activation(cv, cv, AF.Silu), reads=[k_cv], writes=[k_cv])
                    S.op("dve", lambda e: e.tensor_copy(scb, cv), reads=[k_cv], writes=[k_scb])
                    for kc in range(8):
                        for m in range(2):
                            S.op("dve", lambda e, kc=kc, m=m: e.tensor_copy(scB4[:, kc, m, :], cv3[:, kc, m:m + 1].to_broadcast([128, 128])), reads=[k_cv], writes=[k_scB])
                    mws = [self.Bm(8 * D) for _ in range(2)]
                    for gi in range(6):
                        mw, k_mw = mws[gi % 2]
                        mw3 = mw.rearrange("p (k n) -> p k n", k=8)
                        for kc in range(8):
                            S.dma("pool", mw3[:, kc, :], mod_w[l][kc * 128:(kc + 1) * 128, gi * D:(gi + 1) * D], writes=[(k_mw, kc)])
                        for oc in range(8):
                            j = gi * 8 + oc
                            for kc in range(8):
                                self.mm(ps[0][:, 2 * j:2 * j + 2], mw3[:, kc, oc * 128:(oc + 1) * 128], scb3[:, kc, :], kc == 0, kc == 7,
                                        reads=[(k_mw, kc), k_scb], writes=[("ps0m", j)])
                        if gi in (2, 5):
                            gidx = 1 if gi == 2 else 2
                            S.dma("sp", mbbc, mod_b[l, gi * D:(gi + 1) * D].partition_broadcast(128), writes=[k_mbbc])
                            for m in range(2):
                                if last and m == 1:
                                    continue
                                gt, k_gt = gate[gidx, m]
                                for h2 in range(2):
                                    for kc in range(8):
                                        self.mm(ps[1 + h2][:], scB4[:, kc, m, :], mw3[:, kc, h2 * 512:(h2 + 1) * 512], kc == 0, kc == 7,
                                                reads=[(k_mw, kc), k_scB], writes=[PK[1 + h2]])
                                    S.op("dve", lambda e, gt=gt, h2=h2: e.tensor_tensor(gt[:, h2 * 512:(h2 + 1) * 512], ps[1 + h2][:], mbbc[:, h2 * 512:(h2 + 1) * 512], ALU.add),
                                         reads=[PK[1 + h2], k_mbbc], writes=[k_gt])
                    S.op("dve", lambda e: e.tensor_tensor(modT3, ps[0][:, 0:96].rearrange("p (j m) -> p j m", m=2), mbT.unsqueeze(2).to_broadcast([128, 48, 2]), ALU.add),
                         reads=[("ps0m", j) for j in range(48)] + [k_mbT], writes=[k_modT])
                    for (Ax, k_Ax, Bx, k_Bx, gs, k_gs, base) in ((A1, k_A1, B1, k_B1, g1s, k_g1s, 0), (A2, k_A2, B2, k_B2, g2s, k_g2s, 24)):
                        S.op("dve", lambda e, Ax=Ax, gs=gs, base=base: e.scalar_tensor_tensor(Ax, modT3[:, base + 8:base + 16, :], 1.0, gs.unsqueeze(2).to_broadcast([128, 8, 2]), ALU.add, ALU.mult),
                             reads=[k_modT, k_gs], writes=[k_Ax])
                        S.op("dve", lambda e, Bx=Bx, base=base: e.tensor_copy(Bx, modT3[:, base:base + 8, :]), reads=[k_modT], writes=[k_Bx])
                import os
                if os.environ.get("SKIPM"):
                    for (tl_, k_) in ((A1, k_A1), (A2, k_A2)):
                        S.op("pool", lambda e, tl_=tl_: e.memset(tl_, 1.0), writes=[k_])
                    for (tl_, k_) in ((B1, k_B1), (B2, k_B2)):
                        S.op("pool", lambda e, tl_=tl_: e.memset(tl_, 0.0), writes=[k_])
                    for kk_ in gate:
                        S.op("pool", lambda e, kk_=kk_: e.memset(gate[kk_][0], 0.5), writes=[gate[kk_][1]])
                else:
                    phaseM()
                self.phase_end()
                self.chk("M%d" % l)

                hT, k_hT = self.Bm(8 * NT)
                hT3 = hT.rearrange("p (k n) -> p k n", k=8)
                mark = self.off
                xts = [self.Fm(D) for _ in range(2)]
                junk, k_junk = self.Fm(D)
                xns = [self.Bm(D) for _ in range(2)]
                ssA, k_ssA = self.Fm(34)
                rsA, k_rsA = self.Fm(34)
                S.op("pool", lambda e: e.memset(ssA, 0.0), writes=[k_ssA])

                def norm_tile(xt, k_xt, ss, k_ss, rs, k_rs, xn, k_xn, Aa, k_Aa, Bb, k_Bb, m, dst3, k_dst, tcol, psb_i):
                    S.op("act", lambda e: e.activation(junk, xt, AF.Square, accum_out=ss), reads=[k_xt, k_ss], writes=[k_junk, k_ss])
                    rstd_from_ss(ss, k_ss, rs, k_rs, 1.0 / D)
                    S.op("dve", lambda e: e.tensor_scalar(xn, xt, rs, None, ALU.mult), reads=[k_xt, k_rs], writes=[k_xn])
                    for kc in range(8):
                        pi = psb_i + (kc // 4)
                        S.op("pe", lambda e, kc=kc, pi=pi: e.matmul(ps[pi][:, (kc % 4) * 128:(kc % 4 + 1) * 128], xn[:, kc * 128:(kc + 1) * 128], ident, start=True, stop=True),
                             reads=[k_xn, k_ident], writes=[PK[pi]])
                    if int(os.environ.get("KA", "3")) < 3:
                        return
                    for kc in range(8):
                        pi = psb_i + (kc // 4)
                        src = ps[pi][:, (kc % 4) * 128:(kc % 4 + 1) * 128]
                        KE = os.environ.get("KE", "act")
                        if (kc % 2 == 0 and KE != "dve") or KE == "act":
                            S.op("act", lambda e, kc=kc, src=src: e.activation(dst3[:, kc, tcol:tcol + 128], src, AF.Identity,
                                                                      bias=Bb[:, kc, m:m + 1], scale=Aa[:, kc, m:m + 1]),
                                 reads=[PK[pi], k_Aa, k_Bb], writes=[(k_dst, tcol, kc)])
                        else:
                            S.op("dve", lambda e, kc=kc, src=src: e.tensor_scalar(dst3[:, kc, tcol:tcol + 128], src,
                                                                        Aa[:, kc, m:m + 1], Bb[:, kc, m:m + 1], ALU.mult, ALU.add),
                                 reads=[PK[pi], k_Aa, k_Bb], writes=[(k_dst, tcol, kc)])

                import os
                for tt in range(int(os.environ.get("KTILES", "34"))):
                    xt, k_xt = xts[tt % 2]
                    xn, k_xn = xns[tt % 2]
                    S.dma("sp", xt, Xl[tt * 128:(tt + 1) * 128, :], reads=[("X", l, tt)], writes=[k_xt])
                    norm_tile(xt, k_xt, ssA[:, tt:tt + 1], (k_ssA, tt), rsA[:, tt:tt + 1], (k_rsA, tt), xn, k_xn,
                              A1, k_A1, B1, k_B1, 1 if tt < 2 else 0, hT3, k_hT, tt * 128, 2 * (tt % 2))
                S.barrier()
                self.chk("A%d" % l)
                self.alloc_reset(mark)

                def hT_keys(t0, n):
                    return [(k_hT, tc, kc) for tc in range(t0, t0 + n, 128) for kc in range(8)]

                wts = [self.Bm(8 * 128) for _ in range(2)]
                ufs = [self.Fm(NT) for _ in range(2)]
                ucv, k_ucv = self.Fm(NT)
                cw, k_cw = self.Fm(18); cb, k_cb = self.Fm(6)
                cw3 = cw.rearrange("p (c t) -> p c t", t=3)
                S.dma("sp", cw3, cwT[l], writes=[k_cw])
                S.dma("sp", cb, cbT[l], writes=[k_cb])
                vst, k_vst = self.Bm(34 * 128)
                vst3 = vst.rearrange("p (t c) -> p t c", c=128)
                for oc in range(15):
                    mw = 128 if oc < 14 else 32
                    wt, k_wt = wts[oc % 2]
                    wt3 = wt.rearrange("p (k m) -> p k m", k=8)
                    uf, k_uf = ufs[oc % 2]
                    S.dma("sp", wt3[:, :, 0:mw], winS[l, oc, :, :, 0:mw], reads=[("winS", l, oc)], writes=[k_wt])
                    for ci, (t0, n) in enumerate(tok_chunks):
                        pi = 2 + (ci % 4)
                        for kc in range(8):
                            self.mm(ps[pi][0:mw, 0:n], wt3[:, kc, 0:mw], hT3[:, kc, t0:t0 + n], kc == 0, kc == 7,
                                    reads=[k_wt] + hT_keys(t0, n), writes=[PK[pi]])
                        copy_op(evac_eng(), uf[0:mw, t0:t0 + n], ps[pi][0:mw, 0:n], [PK[pi]], [(k_uf, ci)])
                    ufk = [(k_uf, ci) for ci in range(9)]
                    if oc < 6:
                        S.op("act", lambda e, uf=uf, oc=oc: e.activation(ucv, uf, AF.Identity, bias=cb[:, oc:oc + 1], scale=cw3[:, oc, 1:2]),
                             reads=ufk + [k_cw, k_cb], writes=[k_ucv])
                        for (a, b) in ((0, NCX), (NCX, NT)):
                            S.op("dve", lambda e, uf=uf, oc=oc, a=a, b=b: e.scalar_tensor_tensor(ucv[:, a + 1:b], uf[:, a:b - 1], cw3[:, oc, 0:1], ucv[:, a + 1:b], ALU.mult, ALU.add),
                                 reads=ufk + [k_cw, k_ucv], writes=[k_ucv])
                            S.op("dve", lambda e, uf=uf, oc=oc, a=a, b=b: e.scalar_tensor_tensor(ucv[:, a:b - 1], uf[:, a + 1:b], cw3[:, oc, 2:3], ucv[:, a:b - 1], ALU.mult, ALU.add),
                                 reads=ufk + [k_cw, k_ucv], writes=[k_ucv])
                        S.dma("sp", hyT[oc * 128:(oc + 1) * 128, :], ucv, reads=[k_ucv], writes=[("hyT", oc)])
                    else:
                        r0 = (oc - 6) * 128
                        S.dma("sp", uT[r0:r0 + mw, :], uf[0:mw, :], reads=ufk, writes=[("uT", oc)])
                    if oc == 10:
                        for tt in range(34):
                            pi = 6 + (tt % 2)
                            for kc in range(8):
                                self.mm(ps[pi][:, 0:128], hT3[:, kc, tt * 128:(tt + 1) * 128], wt3[:, kc, :], kc == 0, kc == 7,
                                        reads=[k_wt] + hT_keys(tt * 128, 128), writes=[PK[pi]])
                            copy_op(evac_eng(), vst3[:, tt, :], ps[pi][:, 0:128], [PK[pi]], [(k_vst, tt)])
                        S.dma("sp", Vall[:, 0:2, :].rearrange("(t p) h d -> p t (h d)", p=128), vst3, reads=[(k_vst, tt) for tt in range(34)], writes=[("Vall", 0), ("Vall", 1)])
                self.phase_end()
                self.chk("P2%d" % l)

                streams = [(4096, NCX, 64)] + ([] if last else [(256, 0, 4)])
                for (L, toff, N1) in streams:
                    self.hyena(l, L, toff, fc[N1], M2S[N1], D2S[N1], N1, dict(
                        zT=zT[L], tl=tl[L], ndelta=ndelta, fw1=fw1, fb1=fb1, fw2=fw2, fb2=fb2, fw3=fw3, ffr=ffr, hbT=hbT,
                        hyT=hyT, cvy=cvy, cvz=cvz, kTs=kTs, yT=yT, rn=rn, k_rn=k_rn, hb=hb, k_hb=k_hb, ident=ident, k_ident=k_ident))

                self.chk("H%d" % l)
                self.alloc_reset()
                cG, k_cG = self.Fm(NT); sG, k_sG = self.Fm(NT)
                cM, k_cM = self.Fm(NT); sM, k_sM = self.Fm(NT)
                S.dma("sp", cG, cosG, writes=[k_cG]); S.dma("sp", sG, sinG, writes=[k_sG])
                S.dma("sp", cM[0:96, :], cosM, writes=[k_cM]); S.dma("sp", sM[0:96, :], sinM, writes=[k_sM])
                pgf, k_pgf = self.Fm(128); pg, k_pg = self.Bm(128)
                pmf, k_pmf = self.Fm(96); pm, k_pm = self.Bm(96)
                S.dma("sp", pgf, PGd, writes=[k_pgf]); S.dma("sp", pmf[0:96, :], PMd, writes=[k_pmf])
                S.op("dve", lambda e: e.tensor_copy(pg, pgf), reads=[k_pgf], writes=[k_pg])
                S.op("dve", lambda e: e.tensor_copy(pm[0:96, :], pmf[0:96, :]), reads=[k_pmf], writes=[k_pm])
                gq_s, k_gq = self.Fm(1); gk_s, k_gk = self.Fm(1); mqg_s, k_mqg = self.Fm(2); mkvg_s, k_mkvg = self.Fm(1)
                S.dma("sp", gq_s, gqT[l], writes=[k_gq]); S.dma("sp", gk_s, gkT[l], writes=[k_gk])
                S.dma("sp", mqg_s, mqgT[l], writes=[k_mqg]); S.dma("sp", mkvg_s, mkvgT[l], writes=[k_mkvg])
                wuq, k_wuq = self.Bm(2 * 576); wuq3 = wuq.rearrange("p (k n) -> p k n", k=2)
                wukv, k_wukv = self.Bm(768); wukv3 = wukv.rearrange("p (h c) -> p h c", c=128)
                S.dma("sp", wuq3, wuqS[l], reads=[("wuqS", l)], writes=[k_wuq])
                S.dma("sp", wukv, wukvS[l], reads=[("wukvS", l)], writes=[k_wukv])
                NB = 2
                xin_b = [self.Fm(512) for _ in range(NB)]
                sq_b = [self.Fm(512) for _ in range(NB)]
                rs_b = [self.Fm(512) for _ in range(NB)]
                xn_b = [self.Fm(512) for _ in range(NB)]
                xnb_b = [self.Bm(512) for _ in range(NB)]
                t1_b = [self.Fm(512) for _ in range(NB)]
                t2_b = [self.Fm(512) for _ in range(NB)]
                ob_b = [self.Bm(512) for _ in range(NB)]
                mq2 = [self.Fm(1024) for _ in range(1)]
                cqn, k_cqn = self.Bm(1024); cqn3 = cqn.rearrange("p (c n) -> p c n", c=2)
                ckvn, k_ckvn = self.Bm(512)
                vms, k_vms = self.Bm(4 * 384)
                vms3 = vms.rearrange("p (s c) -> p s c", s=4)
                cnt = [0]

                def rope_out(x32, k_x32, nrow, n, t0, pmat, k_pmat, cT, k_cT, sT, k_sT, dsts):
                    b = cnt[0] % NB
                    xnb, k_xnb = xnb_b[b]; t1, k_t1 = t1_b[b]; t2, k_t2 = t2_b[b]; ob, k_ob = ob_b[b]
                    pi = 4 + (cnt[0] % 2)
                    S.op("dve", lambda e: e.tensor_copy(xnb[0:nrow, 0:n], x32), reads=[k_x32], writes=[k_xnb])
                    self.mm(ps[pi][0:nrow, 0:n], pmat[0:nrow, 0:nrow], xnb[0:nrow, 0:n], True, True, reads=[k_pmat, k_xnb], writes=[PK[pi]])
                    S.op("pool", lambda e: e.tensor_tensor(t1[0:nrow, 0:n], x32, cT[0:nrow, t0:t0 + n], ALU.mult), reads=[k_x32, k_cT], writes=[k_t1])
                    S.op("dve", lambda e: e.tensor_tensor(t2[0:nrow, 0:n], ps[pi][0:nrow, 0:n], sT[0:nrow, t0:t0 + n], ALU.mult), reads=[PK[pi], k_sT], writes=[k_t2])
                    S.op("pool", lambda e: e.tensor_tensor(ob[0:nrow, 0:n], t1[0:nrow, 0:n], t2[0:nrow, 0:n], ALU.add), reads=[k_t1, k_t2], writes=[k_ob])
                    for (dst, r0, r1, key) in dsts:
                        S.dma("sp", dst, ob[r0:r1, 0:n], reads=[k_ob], writes=[key])
                    cnt[0] += 1

                def pnorm(xs, nchunk, n, ones_m, k_ones, inv_d, gsc, k_gsc, outs):
                    b = cnt[0] % NB
                    sq, k_sq = sq_b[b]; rs, k_rs = rs_b[b]
                    pi = 6 + (cnt[0] % 2)
                    for c, (x, k_x) in enumerate(xs):
                        S.op("act", lambda e, x=x: e.activation(sq[:, 0:n], x, AF.Square), reads=[k_x], writes=[k_sq])
                        self.mm(ps[pi][:, 0:n], ones_m, sq[:, 0:n], c == 0, c == nchunk - 1, reads=[k_ones, k_sq], writes=[PK[pi]])
                    S.op("act", lambda e: e.activation(rs[:, 0:n], ps[pi][:, 0:n], AF.Sqrt, bias=epsc, scale=inv_d), reads=[PK[pi], k_eps], writes=[k_rs])
                    S.op("dve", lambda e: e.reciprocal(rs[:, 0:n], rs[:, 0:n]), reads=[k_rs], writes=[k_rs])
                    for c, ((x, k_x), (o, k_o)) in enumerate(zip(xs, outs)):
                        S.op("dve", lambda e, x=x, o=o, c=c: e.scalar_tensor_tensor(o, x, gsc[:, c:c + 1], rs[:, 0:n], ALU.mult, ALU.mult), reads=[k_x, k_gsc, k_rs], writes=[k_o])

                for ci, (t0, n) in enumerate(tok_chunks):
                    isctx = ci == 0
                    need_q = not (last and isctx)
                    for j in range(4):
                        if j < 3 and not need_q:
                            continue
                        b = cnt[0] % NB
                        xi, k_xi = xin_b[b]; xn, k_xn = xn_b[b]
                        S.dma("sp", xi[:, 0:n], uT[j * 128:(j + 1) * 128, t0:t0 + n], reads=[("uT", 6 + j)], writes=[k_xi])
                        pnorm([(xi[:, 0:n], k_xi)], 1, n, bd64, k_bd64, 1.0 / 64, gq_s if j < 3 else gk_s, k_gq if j < 3 else k_gk, [(xn[:, 0:n], k_xn)])
                        if j < 3:
                            dsts = [(QT[2 * j, 0:64, t0:t0 + n], 0, 64, ("QT", 2 * j, ci)), (QT[2 * j + 1, 0:64, t0:t0 + n], 64, 128, ("QT", 2 * j + 1, ci))]
                        else:
                            dsts = [(KT[0, 0:64, t0:t0 + n], 0, 64, ("KT", 0, ci)), (KT[1, 0:64, t0:t0 + n], 64, 128, ("KT", 1, ci))]
                        rope_out(xn[:, 0:n], k_xn, 128, n, t0, pg, k_pg, cG, k_cG, sG, k_sG, dsts)
                    if need_q:
                        m2, k_m2 = mq2[0]
                        m23 = m2.rearrange("p (c n) -> p c n", c=2)
                        for c in range(2):
                            S.dma("sp", m23[:, c, 0:n], uT[640 + c * 128:640 + (c + 1) * 128, t0:t0 + n], reads=[("uT", 11 + c)], writes=[(k_m2, c)])
                        pnorm([(m23[:, c, 0:n], (k_m2, c)) for c in range(2)], 2, n, onesf, k_onesf, 1.0 / 256, mqg_s, k_mqg,
                              [(cqn3[:, c, 0:n], (k_cqn, c)) for c in range(2)])
                        cnt[0] += 1
                        for h in range(6):
                            b = cnt[0] % NB
                            xn, k_xn = xn_b[b]
                            pi = 2 + (h % 2)
                            for c in range(2):
                                self.mm(ps[pi][0:96, 0:n], wuq3[:, c, h * 96:(h + 1) * 96], cqn3[:, c, 0:n], c == 0, c == 1,
                                        reads=[k_wuq, (k_cqn, c)], writes=[PK[pi]])
                            copy_op("act", xn[0:96, 0:n], ps[pi][0:96, 0:n], [PK[pi]], [k_xn])
                            rope_out(xn[0:96, 0:n], k_xn, 96, n, t0, pm, k_pm, cM, k_cM, sM, k_sM, [(QT[6 + h, 0:96, t0:t0 + n], 0, 96, ("QT", 6 + h, ci))])
                    b = cnt[0] % NB
                    xi, k_xi = xin_b[b]
                    S.dma("sp", xi[:, 0:n], uT[896:1024, t0:t0 + n], reads=[("uT", 13)], writes=[k_xi])
                    pnorm([(xi[:, 0:n], k_xi)], 1, n, onesf, k_onesf, 1.0 / 128, mkvg_s, k_mkvg, [(ckvn[:, 0:n], k_ckvn)])
                    cnt[0] += 1
                    for h in range(6):
                        b = cnt[0] % NB
                        ob, k_ob = ob_b[b]
                        pi = 2 + (h % 2)
                        self.mm(ps[pi][0:64, 0:n], wukv3[:, h, 0:64], ckvn[:, 0:n], True, True, reads=[k_wukv, k_ckvn], writes=[PK[pi]])
                        copy_op(evac_eng(), ob[0:64, 0:n], ps[pi][0:64, 0:n], [PK[pi]], [k_ob])
                        S.dma("sp", KT[2 + h, 0:64, t0:t0 + n], ob[0:64, 0:n], reads=[k_ob], writes=[("KT", 2 + h, ci, 0)])
                        cnt[0] += 1
                    for s in range(n // 128):
                        pi = 2 + (s % 2)
                        self.mm(ps[pi][:, 0:384].rearrange("p (h d) -> p h d", d=64), ckvn[:, s * 128:(s + 1) * 128], wukv3[:, :, 64:128], True, True,
                                reads=[k_wukv, k_ckvn], writes=[PK[pi]])
                        copy_op(evac_eng(), vms3[:, s, :], ps[pi][:, 0:384], [PK[pi]], [(k_vms, s)])
                    S.dma("sp", Vall[t0:t0 + n, 2:8, :].rearrange("(s p) h d -> p s (h d)", p=128), vms3[:, 0:n // 128, :],
                          reads=[(k_vms, s) for s in range(n // 128)], writes=[("Vall", 2 + h, ci) for h in range(6)])
                    b = cnt[0] % NB
                    xn, k_xn = xn_b[b]
                    S.op("pool", lambda e, xn=xn: e.memset(xn[0:64, 0:n], 0.0), writes=[k_xn])
                    S.dma("sp", xn[64:96, 0:n], uT[1024:1056, t0:t0 + n], reads=[("uT", 14)], writes=[(k_xn, "pe")])
                    S.op("pool", lambda e: e.nop(), reads=[(k_xn, "pe")], writes=[k_xn])
                    rope_out(xn[0:96, 0:n], k_xn, 96, n, t0, pm, k_pm, cM, k_cM, sM, k_sM,
                             [(KT[2 + h, 64:96, t0:t0 + n], 64, 96, ("KT", 2 + h, ci, 1)) for h in range(6)])
                self.phase_end()
                self.chk("Q%d" % l)

                kts = [self.Bm(NT) for _ in range(2)]
                qts = [self.Bm(NT) for _ in range(2)]
                vts = [self.Bm(34 * 128) for _ in range(2)]
                for b in range(2):
                    v3 = vts[b][0].rearrange("p (t c) -> p t c", c=128)
                    S.op("pool", lambda e, v3=v3: e.memset(v3[:, :, 64:128], 1.0), writes=[(vts[b][1], "ones")])
                pbs = [self.Bm(512) for _ in range(4)]
                rds = [self.Fm(512) for _ in range(2)]
                obs = [self.Bm(512) for _ in range(2)]
                pcnt = [0]
                ocnt = [0]
                for hh in range(12):
                    if hh < 6:
                        dk = 64; kvh = hh // 3; scale = 64 ** -0.5; yrow = 256 + 64 * hh
                    else:
                        dk = 96; kvh = 2 + (hh - 6); scale = 96 ** -0.5; yrow = 640 + 64 * (hh - 6)
                    kt, k_kt = kts[hh % 2]; qt, k_qt = qts[hh % 2]; vt, k_vt = vts[hh % 2]
                    vt3 = vt.rearrange("p (t c) -> p t c", c=128)
                    S.dma("sp", kt[0:dk, :], KT[kvh, 0:dk, :], reads=[("KT", kvh)], writes=[k_kt])
                    S.dma("sp", qt[0:dk, :], QT[hh, 0:dk, :], reads=[("QT", hh)], writes=[k_qt])
                    for tg in range(2):
                        S.dma("sp", vt3[:, tg * 17:(tg + 1) * 17, 0:64], Vall[tg * 17 * 128:(tg + 1) * 17 * 128, kvh, :].rearrange("(t p) d -> p t d", p=128),
                              reads=[("Vall", kvh), (k_vt, "ones")], writes=[(k_vt, tg)])
                    qsets = [(NCX + 512 * i, 512, 0, 34) for i in range(8)]
                    if not last:
                        qsets.append((0, NCX, 0, 2))
                    for (q0, qn, kt0, ktn) in qsets:
                        po = ocnt[0] % 2
                        for kk in range(kt0, kt0 + ktn):
                            pi = 2 + (pcnt[0] % 3)
                            pb, k_pb = pbs[pcnt[0] % 4]
                            self.mm(ps[pi][:, 0:qn], kt[0:dk, kk * 128:(kk + 1) * 128], qt[0:dk, q0:q0 + qn], True, True, reads=[k_kt, k_qt], writes=[PK[pi]])
                            S.op("act", lambda e, pb=pb, pi=pi, qn=qn, scale=scale: e.activation(pb[:, 0:qn], ps[pi][:, 0:qn], AF.Exp, scale=scale), reads=[PK[pi]], writes=[k_pb])
                            self.mm(ps[po][:, 0:qn], vt3[:, kk, :], pb[:, 0:qn], kk == kt0, kk == kt0 + ktn - 1, reads=[(k_vt, kk // 17), (k_vt, "ones"), k_pb], writes=[PK[po]])
                            pcnt[0] += 1
                        rd, k_rd = rds[ocnt[0] % 2]; ob, k_ob = obs[ocnt[0] % 2]
                        S.op("dve", lambda e, rd=rd, po=po, qn=qn: e.reciprocal(rd[64:128, 0:qn], ps[po][64:128, 0:qn]), reads=[PK[po]], writes=[k_rd])
                        S.op("dve", lambda e, rd=rd, ob=ob, po=po, qn=qn: e.tensor_tensor(ob[0:64, 0:qn], ps[po][0:64, 0:qn], rd[64:128, 0:qn], ALU.mult), reads=[PK[po], k_rd], writes=[k_ob])
                        S.dma("sp", yT[yrow:yrow + 64, q0:q0 + qn], ob[0:64, 0:qn], reads=[k_ob], writes=[("yT", yrow, q0)])
                        ocnt[0] += 1
                self.phase_end()
                self.chk("T%d" % l)

                wo, k_wo = self.Bm(8 * D); wo3 = wo.rearrange("p (k n) -> p k n", k=8)
                w2, k_w2 = self.Bm(NJ * D); w23 = w2.rearrange("p (j n) -> p j n", j=NJ)
                S.dma("sp", wo3, woutS[l], reads=[("woutS", l, 0), ("woutS", l, 1)], writes=[k_wo])
                S.dma("sp", w23, w2S[l], reads=[("w2S", l, 0), ("w2S", l, 1)], writes=[k_w2])
                TGm = 512
                yTs, k_yTs = self.Bm(8 * TGm); yTs3 = yTs.rearrange("p (k n) -> p k n", k=8)
                x1, k_x1 = self.Fm(4 * D); x13 = x1.rearrange("p (i n) -> p i n", i=4)
                h2T, k_h2T = self.Bm(8 * TGm); h2T3 = h2T.rearrange("p (k n) -> p k n", k=8)
                actT, k_actT = self.Bm(NJ * TGm); actT3 = actT.rearrange("p (j n) -> p j n", j=NJ)
                xts = [self.Fm(D) for _ in range(2)]
                tmp, k_tmp = self.Fm(D)
                junk, k_junk = self.Fm(D)
                xns = [self.Bm(D) for _ in range(2)]
                ssO, k_ssO = self.Fm(80); rsO, k_rsO = self.Fm(80)
                S.op("pool", lambda e: e.memset(ssO, 0.0), writes=[k_ssO])
                w1ts = [self.Bm(8 * 128) for _ in range(2)]
                w3ts = [self.Bm(8 * 128) for _ in range(2)]
                sil = [self.Fm(512) for _ in range(2)]
                groups = [(NCX + TGm * i, TGm, 0) for i in range(NL // TGm)]
                if not last:
                    groups.append((0, NCX, 1))
                tcount = [0]
                jcount = [0]
                for (g0, gn, m) in groups:
                    nti = gn // 128
                    S.dma("sp", yTs3[:, :, 0:gn], yT[:, g0:g0 + gn].rearrange("(k p) n -> p k n", p=128),
                          reads=[("yT",)], writes=[k_yTs])
                    gt1, k_gt1 = gate[1, m]; gt2, k_gt2 = gate[2, m]
                    for i in range(nti):
                        tc = tcount[0]; tcount[0] += 1
                        xt, k_xt = xts[tc % 2]; xn, k_xn = xns[tc % 2]
                        pa = 0 if tc % 2 == 0 else 2
                        S.dma("sp", xt, Xl[g0 + i * 128:g0 + (i + 1) * 128, :], reads=[("X", l, (g0 + i * 128) // 128)], writes=[k_xt])
                        for h2 in range(2):
                            for kc in range(8):
                                self.mm(ps[pa + h2][:], yTs3[:, kc, i * 128:(i + 1) * 128], wo3[:, kc, h2 * 512:(h2 + 1) * 512], kc == 0, kc == 7,
                                        reads=[k_yTs, k_wo], writes=[PK[pa + h2]])
                            S.op("dve", lambda e, pa=pa, h2=h2, gt1=gt1: e.tensor_tensor(tmp[:, h2 * 512:(h2 + 1) * 512], ps[pa + h2][:], gt1[:, h2 * 512:(h2 + 1) * 512], ALU.mult),
                                 reads=[PK[pa + h2], k_gt1], writes=[(k_tmp, h2)])
                        S.op("pool", lambda e, i=i, xt=xt: e.tensor_tensor(x13[:, i, :], tmp, xt, ALU.add), reads=[(k_tmp, 0), (k_tmp, 1), k_xt], writes=[(k_x1, i)])
                        norm_tile_O = norm_tile
                        S.op("act", lambda e, i=i, tc=tc: e.activation(junk, x13[:, i, :], AF.Square, accum_out=ssO[:, tc:tc + 1]), reads=[(k_x1, i), k_ssO], writes=[k_junk, (k_ssO, tc)])
                        rstd_from_ss(ssO[:, tc:tc + 1], (k_ssO, tc), rsO[:, tc:tc + 1], (k_rsO, tc), 1.0 / D)
                        S.op("dve", lambda e, i=i, tc=tc, xn=xn: e.tensor_scalar(xn, x13[:, i, :], rsO[:, tc:tc + 1], None, ALU.mult), reads=[(k_x1, i), (k_rsO, tc)], writes=[k_xn])
                        pb0 = 4 + 2 * (tc % 2)
                        for kc in range(8):
                            pi = pb0 + (kc // 4)
                            S.op("pe", lambda e, kc=kc, pi=pi, xn=xn: e.matmul(ps[pi][:, (kc % 4) * 128:(kc % 4 + 1) * 128], xn[:, kc * 128:(kc + 1) * 128], ident, start=True, stop=True),
                                 reads=[k_xn, k_ident], writes=[PK[pi]])
                        for kc in range(8):
                            pi = pb0 + (kc // 4)
                            src = ps[pi][:, (kc % 4) * 128:(kc % 4 + 1) * 128]
                            if True:
                                S.op("act", lambda e, kc=kc, src=src, i=i, m=m: e.activation(h2T3[:, kc, i * 128:(i + 1) * 128], src, AF.Identity,
                                                                                          bias=B2[:, kc, m:m + 1], scale=A2[:, kc, m:m + 1]),
                                     reads=[PK[pi], k_A2, k_B2], writes=[(k_h2T, i, kc)])
                            else:
                                S.op("dve", lambda e, kc=kc, src=src, i=i, m=m: e.tensor_scalar(h2T3[:, kc, i * 128:(i + 1) * 128], src,
                                                                                            A2[:, kc, m:m + 1], B2[:, kc, m:m + 1], ALU.mult, ALU.add),
                                     reads=[PK[pi], k_A2, k_B2], writes=[(k_h2T, i, kc)])
                    h2keys = [(k_h2T, i, kc) for i in range(nti) for kc in range(8)]
                    for j in range(NJ):
                        jc = jcount[0]; jcount[0] += 1
                        w1t, k_w1t = w1ts[jc % 2]; w3t, k_w3t = w3ts[jc % 2]
                        w1t3 = w1t.rearrange("p (k m) -> p k m", k=8); w3t3 = w3t.rearrange("p (k m) -> p k m", k=8)
                        S.dma("sp", w1t3, w1S[l, j], reads=[("w1S", l, j)], writes=[k_w1t])
                        S.dma("sp", w3t3, w3S[l, j], reads=[("w3S", l, j)], writes=[k_w3t])
                        p1 = 4 + (jc % 2); p3 = 6 + (jc % 2)
                        sl, k_sl = sil[jc % 2]
                        for kc in range(8):
                            self.mm(ps[p1][:, 0:gn], w1t3[:, kc, :], h2T3[:, kc, 0:gn], kc == 0, kc == 7, reads=[k_w1t] + h2keys, writes=[PK[p1]])
                        for kc in range(8):
                            self.mm(ps[p3][:, 0:gn], w3t3[:, kc, :], h2T3[:, kc, 0:gn], kc == 0, kc == 7, reads=[k_w3t] + h2keys, writes=[PK[p3]])
                        S.op("act", lambda e, sl=sl, p1=p1, gn=gn: e.activation(sl[:, 0:gn], ps[p1][:, 0:gn], AF.Silu), reads=[PK[p1]], writes=[k_sl])
                        S.op("dve", lambda e, sl=sl, p3=p3, gn=gn, j=j: e.tensor_tensor(actT3[:, j, 0:gn], sl[:, 0:gn], ps[p3][:, 0:gn], ALU.mult), reads=[k_sl, PK[p3]], writes=[(k_actT, j)])
                    akeys = [(k_actT, j) for j in range(NJ)]
                    for i in range(nti):
                        tc = tcount[0]; tcount[0] += 1
                        xt, k_xt = xts[tc % 2]
                        pa = 0 if tc % 2 == 0 else 2
                        for h2 in range(2):
                            for j in range(NJ):
                                self.mm(ps[pa + h2][:], actT3[:, j, i * 128:(i + 1) * 128], w23[:, j, h2 * 512:(h2 + 1) * 512], j == 0, j == NJ - 1,
                                        reads=akeys + [k_w2], writes=[PK[pa + h2]])
                            S.op("dve", lambda e, pa=pa, h2=h2, gt2=gt2: e.tensor_tensor(tmp[:, h2 * 512:(h2 + 1) * 512], ps[pa + h2][:], gt2[:, h2 * 512:(h2 + 1) * 512], ALU.mult),
                                 reads=[PK[pa + h2], k_gt2], writes=[(k_tmp, h2)])
                        S.op("pool", lambda e, i=i, xt=xt: e.tensor_tensor(xt, tmp, x13[:, i, :], ALU.add), reads=[(k_tmp, 0), (k_tmp, 1), (k_x1, i)], writes=[k_xt])
                        r0 = g0 + i * 128
                        if not last:
                            S.dma("sp", xres[r0:r0 + 128, :], xt, reads=[k_xt], writes=[("X", l + 1, r0 // 128)])
                        else:
                            S.op("act", lambda e, xt=xt, tc=tc: e.activation(junk, xt, AF.Square, accum_out=ssO[:, tc:tc + 1]), reads=[k_xt, k_ssO], writes=[k_junk, (k_ssO, tc)])
                            rstd_from_ss(ssO[:, tc:tc + 1], (k_ssO, tc), rsO[:, tc:tc + 1], (k_rsO, tc), 1.0 / D)
                            S.op("dve", lambda e, xt=xt, tc=tc: e.scalar_tensor_tensor(xt, xt, rsO[:, tc:tc + 1], finalg, ALU.mult, ALU.mult), reads=[k_xt, (k_rsO, tc), k_fing], writes=[k_xt])
                            S.dma("sp", yout[r0 - NCX:r0 - NCX + 128, :], xt, reads=[k_xt], writes=[("yout", r0)])
                self.phase_end()
                self.chk("O%d" % l)


        try:
            self.chk("W")
            run_layers()
        except StopBuild:
            S.barrier()
        S.emit()
        return nc

    def hyena(self, l, L, toff, cfg, M2S, D2S, N1, a):
        nc, S, ps = self.nc, self.S, self.ps
        PK = [("ps", i) for i in range(8)]
        T, NK1 = cfg["T"], cfg["NK1"]
        K2 = 2 * NK1
        self.alloc_reset()
        rn, k_rn, hb, k_hb = a["rn"], a["k_rn"], a["hb"], a["k_hb"]
        ident, k_ident = a["ident"], a["k_ident"]
        LC = min(L, 512)
        nlc = L // LC
        zt, k_zt = self.Fm(L)
        S.dma("sp", zt[0:33, :], a["zT"], writes=[k_zt])
        w1s, k_w1s = self.Fm(64); w2s, k_w2s = self.Fm(64); w3s, k_w3s = self.Fm(1024)
        b1s, k_b1s = self.Fm(1); b2s, k_b2s = self.Fm(1); frs, k_frs = self.Fm(1)
        fb1s, k_fb1s = self.Fm(1); fb2s, k_fb2s = self.Fm(1)
        S.dma("sp", w1s[0:33, :], a["fw1"][l], writes=[k_w1s]); S.dma("sp", w2s[0:64, :], a["fw2"][l], writes=[k_w2s])
        S.dma("sp", w3s[0:64, :], a["fw3"][l], writes=[k_w3s])
        S.dma("sp", b1s[0:64, :], a["fb1"][l], writes=[k_b1s]); S.dma("sp", b2s[0:64, :], a["fb2"][l], writes=[k_b2s])
        S.dma("sp", frs[0:64, :], a["ffr"][l], writes=[k_frs])
        S.dma("sp", hb, a["hbT"][l], writes=[k_hb])
        frp, k_frp = self.Fm(1)
        S.op("dve", lambda e: e.tensor_scalar(frp[0:64, :], frs[0:64, :], float(1.0 / (2 * np.pi)), None, ALU.mult), reads=[k_frs], writes=[k_frp])
        for (bs, k_bs, fb, k_fb) in ((b1s, k_b1s, fb1s, k_fb1s), (b2s, k_b2s, fb2s, k_fb2s)):
            S.op("dve", lambda e, bs=bs, fb=fb: e.tensor_scalar(fb[0:64, :], bs[0:64, :], frp[0:64, :], 64.0, ALU.mult, ALU.add), reads=[k_bs, k_frp], writes=[k_fb])
        h1, k_h1 = self.Fm(L); h2, k_h2 = self.Fm(L)
        u, k_u = self.Fm(LC); ki, k_ki = self.A(LC).bitcast(I32), ("sb", "ki%d" % L)

        def sin_layer(wm, k_wm, kin, src, k_src, fb, k_fb, dst, k_dst):
            for c in range(nlc):
                sl = slice(c * LC, (c + 1) * LC)
                pi = c % 2
                self.mm(ps[pi][0:64, 0:LC], wm[0:kin, 0:64], src[0:kin, sl], True, True, reads=[k_wm, k_src], writes=[PK[pi]])
                S.op("act", lambda e, pi=pi: e.activation(u[0:64, :], ps[pi][0:64, 0:LC], AF.Identity, bias=fb[0:64, :], scale=frp[0:64, :]), reads=[PK[pi], k_frp, k_fb], writes=[k_u])
                S.op("dve", lambda e: e.tensor_copy(ki[0:64, :], u[0:64, :]), reads=[k_u], writes=[k_ki])
                S.op("dve", lambda e: e.tensor_tensor(u[0:64, :], u[0:64, :], ki[0:64, :], ALU.subtract), reads=[k_u, k_ki], writes=[k_u])
                S.op("dve", lambda e: e.scalar_tensor_tensor(u[0:64, :], u[0:64, :], 0.5, u[0:64, :], ALU.is_gt, ALU.subtract), reads=[k_u], writes=[k_u])
                S.op("act", lambda e, sl=sl: e.activation(dst[0:64, sl], u[0:64, :], AF.Sin, scale=float(-2 * np.pi)), reads=[k_u], writes=[(k_dst, c)])
            S.op("pool", lambda e: e.nop(), reads=[(k_dst, c) for c in range(nlc)], writes=[k_dst])

        sin_layer(w1s, k_w1s, 33, zt, k_zt, fb1s, k_fb1s, h1, k_h1)
        sin_layer(w2s, k_w2s, 64, h1, k_h1, fb2s, k_fb2s, h2, k_h2)
        tlb, k_tlb = self.Fm(L)
        S.dma("sp", tlb, a["tl"][0].partition_broadcast(128), writes=[k_tlb])
        nd, k_nd = self.Fm(2)
        S.dma("sp", nd, a["ndelta"], writes=[k_nd])
        dec = [self.Fm(L) for _ in range(2)]
        for hf in range(2):
            S.op("act", lambda e, hf=hf: e.activation(dec[hf][0], tlb, AF.Exp, scale=nd[:, hf:hf + 1]), reads=[k_tlb, k_nd], writes=[dec[hf][1]])
        asum, k_asum = self.Fm(8)
        kTb = [self.Fm(L) for _ in range(2)]
        for cc in range(8):
            o, dr, hf = cc // 4, (cc // 2) % 2, cc % 2
            kt, k_kt = kTb[cc % 2]
            for c in range(nlc):
                sl = slice(c * LC, (c + 1) * LC)
                pi = 2 + (c % 2)
                self.mm(ps[pi][:, 0:LC], w3s[0:64, cc * 128:(cc + 1) * 128], h2[0:64, sl], True, True, reads=[k_w3s, k_h2], writes=[PK[pi]])
                S.op("dve", lambda e, pi=pi, sl=sl, kt=kt, hf=hf: e.tensor_tensor(kt[:, sl], ps[pi][:, 0:LC], dec[hf][0][:, sl], ALU.mult), reads=[PK[pi], dec[hf][1]], writes=[(k_kt, c)])
            S.op("pool", lambda e: e.nop(), reads=[(k_kt, c) for c in range(nlc)], writes=[k_kt])
            if dr == 1:
                S.op("pool", lambda e, kt=kt: e.memset(kt[:, 0:1], 0.0), reads=[k_kt], writes=[k_kt])
            S.op("dve", lambda e, kt=kt, cc=cc: e.tensor_reduce(asum[:, cc:cc + 1], kt, AX.X, ALU.add, apply_absolute_value=True), reads=[k_kt], writes=[(k_asum, cc)])
            S.dma("sp", a["kTs"][cc * 128:(cc + 1) * 128, 0:L], kt, reads=[k_kt], writes=[("kTs", cc)])
        a3 = asum.rearrange("p (o d h) -> p o d h", o=2, d=2)
        rn3 = rn.rearrange("p (o h) -> p o h", o=2)
        S.op("dve", lambda e: e.tensor_tensor(rn3, a3[:, :, 0, :], a3[:, :, 1, :], ALU.add), reads=[(k_asum, cc) for cc in range(8)], writes=[k_rn])
        S.op("dve", lambda e: e.reciprocal(rn, rn), reads=[k_rn], writes=[k_rn])
        S.barrier()
        self.alloc_reset()

        F1f, k_F1f = self.Fm(K2); F1b, k_F1b = self.Bm(K2 + (K2 % 2))
        Gf, k_Gf = self.Fm(T); Gb, k_Gb = self.Bm(T)
        S.dma("sp", F1f[0:T, :], cfg["F1"], writes=[k_F1f]); S.dma("sp", Gf[0:K2, :], cfg["G"], writes=[k_Gf])
        S.op("dve", lambda e: e.tensor_copy(F1b[0:T, 0:K2], F1f[0:T, :]), reads=[k_F1f], writes=[k_F1b])
        S.op("dve", lambda e: e.tensor_copy(Gb[0:K2, :], Gf[0:K2, :]), reads=[k_Gf], writes=[k_Gb])
        fft_mark = self.off
        Ap, k_Ap = self.Bm(NK1 * 2 * 256); Ap4 = Ap.rearrange("p (k r c) -> p k r c", k=NK1, r=2)
        Cp, k_Cp = self.Bm(256 * K2); Cp3 = Cp.rearrange("p (c k) -> p c k", k=K2)
        Kf, k_Kf = self.Bm(NK1 * 2 * 256); Kf4 = Kf.rearrange("p (k r c) -> p k r c", k=NK1, r=2)
        CH = 32
        dts = [self.Bm(CH * 128) for _ in range(2)]
        m2t = [self.Bm(3 * 128) for _ in range(2)]
        d2t = [self.Bm(3 * 128) for _ in range(2)]
        Yb = [self.Bm(512) for _ in range(2)]
        tq = [self.Fm(1024) for _ in range(2)]
        cts = [self.Bm(8 * 128) for _ in range(2)]
        yst = [self.Fm(16 * 128) for _ in range(2)]
        cnt = {"d": 0, "k": 0, "c": 0, "y": 0}
        gat_end = self.off

        def stage1(src, src_key):
            for c0 in range(0, 256, CH):
                dt_, k_dt = dts[cnt["d"] % 2]; cnt["d"] += 1
                dt3 = dt_.rearrange("p (c q) -> p c q", q=128)
                S.dma("pool", dt3[0:T, :, :], src[c0:c0 + CH, :].rearrange("c (t p) -> t c p", p=128), reads=src_key, writes=[k_dt])
                for g in range(CH // 4):
                    pi = g % 2
                    pv = ps[pi][:, 0:4 * K2].rearrange("p (c k) -> p c k", k=K2)
                    for cq in range(4):
                        self.mm(pv[:, cq, :], dt3[0:T, g * 4 + cq, :], F1b[0:T, 0:K2], True, True, reads=[k_dt, k_F1b], writes=[PK[pi]])
                    cc = c0 + g * 4
                    eng = "act" if g % 2 else "dve"
                    copy_op = (lambda e, pv=pv, cc=cc: e.activation(Ap4[:, :, :, cc:cc + 4].rearrange("p k r c -> p c (k r)"), pv, AF.Copy)) if eng == "act" else \
                              (lambda e, pv=pv, cc=cc: e.tensor_copy(Ap4[:, :, :, cc:cc + 4].rearrange("p k r c -> p c (k r)"), pv))
                    S.op(eng, copy_op, reads=[PK[pi]], writes=[(k_Ap, cc // 4)])
            S.op("pool", lambda e: e.nop(), reads=[(k_Ap, i) for i in range(64)], writes=[k_Ap])

        def stage2(k1):
            m2, k_m2 = m2t[cnt["k"] % 2]
            m23 = m2.rearrange("p (v q) -> p v q", v=3)
            S.dma("sp", m23, M2S[k1], reads=[("M2S", N1, k1)], writes=[k_m2])
            pi = 2 + (cnt["k"] % 2)
            X = ps[pi][:].rearrange("p (r c) -> p r c", r=2)
            self.mm(X[:, 0, :], m23[:, 0, :], Ap4[:, k1, 0, :], True, False, reads=[k_m2, k_Ap], writes=[PK[pi]])
            self.mm(X[:, 0, :], m23[:, 2, :], Ap4[:, k1, 1, :], False, True, reads=[k_m2, k_Ap], writes=[PK[pi]])
            self.mm(X[:, 1, :], m23[:, 1, :], Ap4[:, k1, 0, :], True, False, reads=[k_m2, k_Ap], writes=[PK[pi]])
            self.mm(X[:, 1, :], m23[:, 0, :], Ap4[:, k1, 1, :], False, True, reads=[k_m2, k_Ap], writes=[PK[pi]])
            return X, pi

        def filt_fft(o):
            for dr in range(2):
                r0 = (o * 2 + dr) * 256
                stage1(a["kTs"][r0:r0 + 256, 0:L], [("kTs", r0 // 128), ("kTs", r0 // 128 + 1)])
                for k1 in range(NK1):
                    X, pi = stage2(k1)
                    cnt["k"] += 1
                    if dr == 0:
                        S.op("act", lambda e, X=X, k1=k1: e.activation(Kf4[:, k1, :, :], X, AF.Copy), reads=[PK[pi]], writes=[(k_Kf, k1)])
                    else:
                        S.op("dve", lambda e, X=X, k1=k1: e.tensor_tensor(Kf4[:, k1, 0, :], Kf4[:, k1, 0, :], X[:, 0, :], ALU.add), reads=[PK[pi], (k_Kf, k1)], writes=[(k_Kf, k1)])
                        S.op("dve", lambda e, X=X, k1=k1: e.tensor_tensor(Kf4[:, k1, 1, :], Kf4[:, k1, 1, :], X[:, 1, :], ALU.subtract), reads=[PK[pi], (k_Kf, k1)], writes=[(k_Kf, k1)])

        def conv(src, src_key, dst, dst_name):
            stage1(src, src_key)
            for k1 in range(NK1):
                X, pi = stage2(k1)
                kq = cnt["k"]; cnt["k"] += 1
                Y, k_Y = Yb[kq % 2]; Y3 = Y.rearrange("p (r c) -> p r c", r=2)
                t4, k_t4 = tq[kq % 2]; t43 = t4.rearrange("p (q c) -> p q c", q=4)
                S.op("dve", lambda e, X=X, k1=k1, t43=t43: e.tensor_tensor(t43[:, 0:2, :], X, Kf4[:, k1, 0:1, :].to_broadcast([128, 2, 256]), ALU.mult), reads=[PK[pi], (k_Kf, k1)], writes=[(k_t4, 0)])
                S.op("dve", lambda e, X=X, k1=k1, t43=t43: e.tensor_tensor(t43[:, 2:4, :], X, Kf4[:, k1, 1:2, :].to_broadcast([128, 2, 256]), ALU.mult), reads=[PK[pi], (k_Kf, k1)], writes=[(k_t4, 1)])
                S.op("pool", lambda e, Y3=Y3, t43=t43: e.tensor_tensor(Y3[:, 0, :], t43[:, 0, :], t43[:, 3, :], ALU.subtract), reads=[(k_t4, 0), (k_t4, 1)], writes=[(k_Y, 0)])
                S.op("pool", lambda e, Y3=Y3, t43=t43: e.tensor_tensor(Y3[:, 1, :], t43[:, 1, :], t43[:, 2, :], ALU.add), reads=[(k_t4, 0), (k_t4, 1)], writes=[(k_Y, 1)])
                d2, k_d2 = d2t[kq % 2]
                d23 = d2.rearrange("p (v q) -> p v q", v=3)
                S.dma("sp", d23, D2S[k1], reads=[("D2S", N1, k1)], writes=[k_d2])
                pc = 4 + (kq % 2)
                C = ps[pc][:].rearrange("p (r c) -> p r c", r=2)
                yk = [(k_Y, 0), (k_Y, 1)]
                self.mm(C[:, 0, :], d23[:, 0, :], Y3[:, 0, :], True, False, reads=[k_d2] + yk, writes=[PK[pc]])
                self.mm(C[:, 0, :], d23[:, 2, :], Y3[:, 1, :], False, True, reads=[k_d2] + yk, writes=[PK[pc]])
                self.mm(C[:, 1, :], d23[:, 0, :], Y3[:, 1, :], True, False, reads=[k_d2] + yk, writes=[PK[pc]])
                self.mm(C[:, 1, :], d23[:, 1, :], Y3[:, 0, :], False, True, reads=[k_d2] + yk, writes=[PK[pc]])
                S.op("act", lambda e, C=C, k1=k1: e.activation(Cp3[:, :, 2 * k1:2 * k1 + 2].rearrange("p c r -> p r c"), C, AF.Copy), reads=[PK[pc]], writes=[(k_Cp, k1)])
            S.op("pool", lambda e: e.nop(), reads=[(k_Cp, k1) for k1 in range(NK1)], writes=[k_Cp])
            for c0 in range(0, 256, 16):
                ys, k_ys = yst[cnt["y"] % 2]; cnt["y"] += 1
                ys3 = ys.rearrange("p (c q) -> p c q", q=128)
                for g4 in range(4):
                    ct, k_ct = cts[cnt["c"] % 2]
                    pt = 6 + (cnt["c"] % 2); cnt["c"] += 1
                    ptv = ps[pt][:].rearrange("p (c q) -> p c q", q=128)
                    ct3 = ct.rearrange("p (c q) -> p c q", q=128)
                    for cq in range(4):
                        c = c0 + g4 * 4 + cq
                        S.op("pe", lambda e, ptv=ptv, cq=cq, c=c: e.matmul(ptv[0:K2, cq, :], Cp3[:, c, :], ident, start=True, stop=True), reads=[k_Cp, k_ident], writes=[PK[pt]])
                    S.op("act" if g4 % 2 else "dve", (lambda e, ct3=ct3, ptv=ptv: e.activation(ct3[0:K2, 0:4, :], ptv[0:K2, :, :], AF.Copy)) if g4 % 2 else
                         (lambda e, ct3=ct3, ptv=ptv: e.tensor_copy(ct3[0:K2, 0:4, :], ptv[0:K2, :, :])), reads=[PK[pt]], writes=[k_ct])
                    py = g4 % 2
                    self.mm(ps[py][0:T, :], Gb[0:K2, 0:T], ct3[0:K2, 0:4, :], True, True, reads=[k_Gb, k_ct], writes=[PK[py]])
                    dstv = ys3[0:T, g4 * 4:g4 * 4 + 4, :]
                    srcv = ps[py][0:T, :].rearrange("p (c q) -> p c q", q=128)
                    if g4 % 2:
                        S.op("act", lambda e, dstv=dstv, srcv=srcv: e.activation(dstv, srcv, AF.Copy), reads=[PK[py]], writes=[(k_ys, g4)])
                    else:
                        S.op("dve", lambda e, dstv=dstv, srcv=srcv: e.tensor_copy(dstv, srcv), reads=[PK[py]], writes=[(k_ys, g4)])
                S.dma("sp", dst[c0:c0 + 16, :].rearrange("c (t p) -> t c p", p=128), ys3[0:T, :, :],
                      reads=[(k_ys, g4) for g4 in range(4)], writes=[(dst_name, c0)])

        def gating(o, xrow, vsrc, vkeys, ysrc, yname, dst_fn, dst_keys):
            mark = fft_mark
            for hf in range(2):
                self.alloc_reset(mark)
                xv, k_xv = self.Fm(L); vv, k_vv = self.Fm(L); yy, k_yy = self.Fm(L)
                r = slice(hf * 128, (hf + 1) * 128)
                S.dma("sp", xv, a["hyT"][xrow + hf * 128:xrow + (hf + 1) * 128, toff:toff + L], reads=[("hyT", (xrow // 128) + hf)], writes=[k_xv])
                S.dma("sp", vv, vsrc[r, toff:toff + L], reads=vkeys, writes=[k_vv])
                S.dma("sp", yy, ysrc[r, toff:toff + L], reads=[(yname, c0) for c0 in range(hf * 128, (hf + 1) * 128, 16)], writes=[k_yy])
                S.op("act", lambda e, yy=yy, hf=hf: e.activation(yy, yy, AF.Copy, scale=rn[:, 2 * o + hf:2 * o + hf + 1]), reads=[k_yy, k_rn], writes=[k_yy])
                S.op("dve", lambda e, yy=yy, vv=vv, hf=hf: e.scalar_tensor_tensor(yy, vv, hb[:, 2 * o + hf:2 * o + hf + 1], yy, ALU.mult, ALU.add), reads=[k_vv, k_hb, k_yy], writes=[k_yy])
                dst_fn(hf, xv, k_xv, yy, k_yy)
                S.barrier()
            self.alloc_reset(gat_end)

        hyT, cvy, cvz, yT = a["hyT"], a["cvy"], a["cvz"], a["yT"]
        filt_fft(0)
        conv(hyT[0:256, toff:toff + L], [("hyT", 0), ("hyT", 1)], cvy[:, toff:toff + L], "cvy")
        S.barrier()

        def dst1(hf, xv, k_xv, yy, k_yy):
            S.op("pool", lambda e: e.tensor_tensor(yy, yy, xv, ALU.mult), reads=[k_yy, k_xv], writes=[k_yy])
            S.dma("sp", cvz[hf * 128:(hf + 1) * 128, toff:toff + L], yy, reads=[k_yy], writes=[("cvz", hf)])
        gating(0, 256, hyT, [("hyT", 0), ("hyT", 1)], cvy, "cvy", dst1, None)
        filt_fft(1)
        conv(cvz[:, toff:toff + L], [("cvz", 0), ("cvz", 1)], cvy[:, toff:toff + L], "cvy")
        S.barrier()

        def dst2(hf, xv, k_xv, yy, k_yy):
            ob, k_ob = self.Bm(L)
            S.op("pool", lambda e: e.tensor_tensor(ob, yy, xv, ALU.mult), reads=[k_yy, k_xv], writes=[k_ob])
            S.dma("sp", yT[hf * 128:(hf + 1) * 128, toff:toff + L], ob, reads=[k_ob], writes=[("yT", hf * 128, toff)])
        gating(1, 512, cvz, [("cvz", 0), ("cvz", 1)], cvy, "cvy", dst2, None)
        self.phase_end()


_CACHE = {}


def _prep_inputs(inp):
    f = lambda a: np.ascontiguousarray(np.asarray(a, dtype=np.float32))
    x, c, ctx, c_ctx = f(inp["x"]), f(inp["c"]), f(inp["ctx"]), f(inp["c_ctx"])
    pk = lambda v, k: np.ascontiguousarray(v.reshape(k, 128).T)
    shared = {}
    shared["mod_w"] = f(inp["mod_w"])
    mod_b = f(inp["mod_b"])
    shared["mod_b"] = mod_b
    shared["modbT"] = np.stack([pk(mod_b[l], 48) for l in range(DEPTH)])
    shared["g1T"] = np.stack([pk(f(inp["norm1_g"])[l], 8) for l in range(DEPTH)])
    shared["g2T"] = np.stack([pk(f(inp["norm2_g"])[l], 8) for l in range(DEPTH)])
    shared["w_in"] = f(inp["w_in"])
    cw = f(inp["hy_conv_w"])
    shared["cwT"] = np.ascontiguousarray(np.stack([np.stack([pk(cw[l, t], 6) for t in range(3)], axis=-1) for l in range(DEPTH)]))
    shared["cbT"] = np.stack([pk(f(inp["hy_conv_b"])[l], 6) for l in range(DEPTH)])
    hbz = f(inp["hy_bias"])
    shared["hbT"] = np.ascontiguousarray(np.stack([np.concatenate([pk(hbz[l, o], 2) for o in range(2)], axis=1) for l in range(DEPTH)]))
    shared["fw1"] = f(inp["hy_filt_w1"]); shared["fb1"] = f(inp["hy_filt_b1"])[:, :, None].copy()
    shared["fw2"] = f(inp["hy_filt_w2"]); shared["fb2"] = f(inp["hy_filt_b2"])[:, :, None].copy()
    shared["fw3"] = f(inp["hy_filt_w3"]); shared["ffr"] = f(inp["hy_filt_freq"])[:, :, None].copy()
    gq = f(inp["gqa_q_g"]); gk = f(inp["gqa_k_g"])
    shared["gqT"] = np.concatenate([gq, gq], axis=1)[:, :, None].copy()
    shared["gkT"] = np.concatenate([gk, gk], axis=1)[:, :, None].copy()
    shared["mqgT"] = np.stack([pk(f(inp["mla_q_g"])[l], 2) for l in range(DEPTH)])
    shared["mkvgT"] = f(inp["mla_kv_g"])[:, :, None].copy()
    shared["w_uq"] = f(inp["mla_w_uq"]); shared["w_ukv"] = f(inp["mla_w_ukv"])
    shared["w_out"] = f(inp["w_out"])
    shared["ffn_w1"] = f(inp["ffn_w1"]); shared["ffn_w3"] = f(inp["ffn_w3"]); shared["ffn_w2"] = f(inp["ffn_w2"])
    shared["fing"] = np.ascontiguousarray(np.broadcast_to(f(inp["final_g"])[None, :], (128, D)))
    if "consts" not in _CACHE:
        cst = {}
        cst["cosG"], cst["sinG"], cst["cosM"], cst["sinM"], cst["PG"], cst["PM"] = rope_tables()
        cst["zT_l"], cst["tl_l"] = hy_embed(4096)
        cst["zT_c"], cst["tl_c"] = hy_embed(256)
        cst["ndelta"] = np.ascontiguousarray((-HY_DELTAS).reshape(2, 128).T).astype(np.float32)
        for tag, N1 in (("l", 64), ("c", 4)):
            F1, M2s, D2s, G = fft_consts(N1)
            cst["F1" + tag], cst["M2" + tag], cst["D2" + tag], cst["G" + tag] = F1, M2s, D2s, G
        _CACHE["consts"] = cst
    shared.update(_CACHE["consts"])
    in_maps = []
    B = x.shape[0]
    for b in range(B):
        m = dict(shared)
        m["xin"] = np.ascontiguousarray(np.concatenate([ctx[b], x[b]], axis=0))
        cc = np.stack([c[b], c_ctx], axis=-1)
        m["cvec"] = np.ascontiguousarray(cc.reshape(8, 128, 2).transpose(1, 0, 2))
        in_maps.append(m)
    return in_maps


def kernel(**inputs):
    in_maps = _prep_inputs(inputs)
    if "nc" not in _CACHE:
        _CACHE["nc"] = KB().build()
    nc = _CACHE["nc"]
    res = run_bass_kernel_spmd(nc, in_maps, core_ids=list(range(len(in_maps))))
    return np.stack([np.asarray(r["yout"], dtype=np.float32) for r in res.results], axis=0)
```
